# Optimizing a Trainium2 kernel written in Bass

```python
import jax, jax.numpy as jnp
from jax import lax
import numpy as np

D_MODEL = 1024
BATCH = 8
SEQ = 8192
DEPTH = 4

N_A_LAYERS = DEPTH // 2
N_B_LAYERS = DEPTH - N_A_LAYERS
D_FF = 2816
FFN_HALF = 0.5
EPS = 1e-6
MEM_TOKENS = 256
MEM_HEADS = 4
MEM_HEAD_DIM = 64
MEM_WIDTH = MEM_HEADS * MEM_HEAD_DIM
MIX_WIDTH = D_MODEL - MEM_WIDTH
CONV_CH = MIX_WIDTH
CONV_WIDTH = 3
MLA_HEADS = 6
QK_NOPE_DIM = 128
QK_ROPE_DIM = 64
QK_HEAD_DIM = QK_NOPE_DIM + QK_ROPE_DIM
V_HEAD_DIM = 128
Q_LORA_RANK = 384
KV_LORA_RANK = 256
ROPE_THETA = 10000.0
Q_BLOCK = 128

kernel_name = "yoco_shortconv_mla_macaron_memory"


def rms_norm(x, g):
    xf = x.astype(jnp.float32)
    y = xf * lax.rsqrt(jnp.mean(xf * xf, axis=-1, keepdims=True) + EPS)
    return (y * g.astype(jnp.float32)).astype(x.dtype)


def swiglu(h, w13, w2):
    gate, up = jnp.split(h @ w13, 2, axis=-1)
    return (jax.nn.silu(gate) * up) @ w2


def rope_tables(positions):
    inv_freq = ROPE_THETA ** (-jnp.arange(0, QK_ROPE_DIM, 2, dtype=jnp.float32) / QK_ROPE_DIM)
    ang = positions.astype(jnp.float32)[..., None] * inv_freq
    return jnp.cos(ang), jnp.sin(ang)


def apply_rope(x, cos, sin):
    x1, x2 = jnp.split(x, 2, axis=-1)
    return jnp.concatenate([x1 * cos - x2 * sin, x2 * cos + x1 * sin], axis=-1).astype(x.dtype)


def causal_short_conv(u, w):
    seq = u.shape[1]
    up = jnp.pad(u, ((0, 0), (CONV_WIDTH - 1, 0), (0, 0)))
    y = w[0] * up[:, 0:seq]
    for tap in range(1, CONV_WIDTH):
        y = y + w[tap] * up[:, tap:tap + seq]
    return y


def short_conv_mixer(h, w_in, conv_w):
    proj = h @ w_in
    gate_b, gate_c, xt, q_mem = jnp.split(proj, [CONV_CH, 2 * CONV_CH, 3 * CONV_CH], axis=-1)
    y = gate_b * causal_short_conv(gate_c * xt, conv_w)
    return y, q_mem


def memory_attention(q_mem, mem, mem_norm_g, w_mem_kv, g_q, g_k):
    b, s, _ = q_mem.shape
    q = rms_norm(q_mem.reshape(b, s, MEM_HEADS, MEM_HEAD_DIM), g_q)
    k, v = jnp.split(rms_norm(mem, mem_norm_g) @ w_mem_kv, 2, axis=-1)
    m = mem.shape[1]
    k = rms_norm(k.reshape(b, m, MEM_HEADS, MEM_HEAD_DIM), g_k)
    v = v.reshape(b, m, MEM_HEADS, MEM_HEAD_DIM)
    scores = jnp.einsum('bqhd,bmhd->bhqm', q, k).astype(jnp.float32) * (MEM_HEAD_DIM ** -0.5)
    p = jax.nn.softmax(scores, axis=-1).astype(v.dtype)
    o = jnp.einsum('bhqm,bmhd->bqhd', p, v)
    return o.reshape(b, s, MEM_WIDTH)


def shared_mla_kv(x, kv_norm_g, w_dkv, g_ckv, w_ukv, w_kr, g_k_nope, g_k_rope, cos, sin):
    b, s, _ = x.shape
    h = rms_norm(x, kv_norm_g)
    c_kv = rms_norm(h @ w_dkv, g_ckv)
    kv = (c_kv @ w_ukv).reshape(b, s, MLA_HEADS, QK_NOPE_DIM + V_HEAD_DIM)
    k_nope = rms_norm(kv[..., :QK_NOPE_DIM], g_k_nope)
    v = kv[..., QK_NOPE_DIM:]
    k_rope = apply_rope(rms_norm(h @ w_kr, g_k_rope), cos, sin)
    return k_nope, k_rope, v


def mla_queries(h, w_in, g_q_lora, w_uq, g_q_nope, g_q_rope, cos, sin):
    b, s, _ = h.shape
    c_q, q_mem = jnp.split(h @ w_in, [Q_LORA_RANK], axis=-1)
    q = (rms_norm(c_q, g_q_lora) @ w_uq).reshape(b, s, MLA_HEADS, QK_HEAD_DIM)
    q_nope = rms_norm(q[..., :QK_NOPE_DIM], g_q_nope)
    q_rope = apply_rope(rms_norm(q[..., QK_NOPE_DIM:], g_q_rope), cos[:, :, None, :], sin[:, :, None, :])
    return q_nope, q_rope, q_mem


def causal_mla_attention(q_nope, q_rope, k_nope, k_rope, v):
    b, seq = q_nope.shape[0], q_nope.shape[1]
    scale = QK_HEAD_DIM ** -0.5
    outs = []
    for blk in range(seq // Q_BLOCK):
        start, end = blk * Q_BLOCK, (blk + 1) * Q_BLOCK
        qn, qr = q_nope[:, start:end], q_rope[:, start:end]
        scores = (jnp.einsum('bqhd,bkhd->bhqk', qn, k_nope[:, :end])
                  + jnp.einsum('bqhd,bkd->bhqk', qr, k_rope[:, :end])).astype(jnp.float32) * scale
        mask = (start + jnp.arange(Q_BLOCK))[:, None] >= jnp.arange(end)[None, :]
        scores = jnp.where(mask, scores, -jnp.inf)
        p = jax.nn.softmax(scores, axis=-1).astype(v.dtype)
        outs.append(jnp.einsum('bhqk,bkhd->bqhd', p, v[:, :end]))
    o = jnp.concatenate(outs, axis=1)
    return o.reshape(b, seq, MLA_HEADS * V_HEAD_DIM)


def setup_inputs(seed: int = 0) -> dict:
    key = jax.random.key(seed)
    ks = jax.random.split(key, 26)
    f32 = jnp.float32

    def w(k, shape, fan_in):
        return jax.random.normal(k, shape, f32) * (fan_in ** -0.5)

    def g(k, shape):
        return 1.0 + 0.02 * jax.random.normal(k, shape, f32)

    offsets = jax.random.randint(ks[2], (BATCH, 1), 0, 1024, dtype=jnp.int32)
    positions = (offsets + jnp.arange(SEQ, dtype=jnp.int32)[None, :]).astype(jnp.int32)
    return {
        'x': jax.random.normal(ks[0], (BATCH, SEQ, D_MODEL), f32),
        'mem': jax.random.normal(ks[1], (BATCH, MEM_TOKENS, D_MODEL), f32),
        'positions': positions,
        'norm_g': g(ks[3], (DEPTH, 3, D_MODEL)),
        'ffn_w13': w(ks[4], (DEPTH, 2, D_MODEL, 2 * D_FF), D_MODEL),
        'ffn_w2': w(ks[5], (DEPTH, 2, D_FF, D_MODEL), D_FF),
        'w_out': w(ks[6], (DEPTH, D_MODEL, D_MODEL), D_MODEL),
        'mem_norm_g': g(ks[7], (DEPTH, D_MODEL)),
        'w_mem_kv': w(ks[8], (DEPTH, D_MODEL, 2 * MEM_WIDTH), D_MODEL),
        'g_mem_q': g(ks[9], (DEPTH, MEM_HEAD_DIM)),
        'g_mem_k': g(ks[10], (DEPTH, MEM_HEAD_DIM)),
        'conv_w_in': w(ks[11], (N_A_LAYERS, D_MODEL, 3 * CONV_CH + MEM_WIDTH), D_MODEL),
        'conv_w': w(ks[12], (N_A_LAYERS, CONV_WIDTH, CONV_CH), CONV_WIDTH),
        'mla_w_in': w(ks[13], (N_B_LAYERS, D_MODEL, Q_LORA_RANK + MEM_WIDTH), D_MODEL),
        'g_q_lora': g(ks[14], (N_B_LAYERS, Q_LORA_RANK)),
        'w_uq': w(ks[15], (N_B_LAYERS, Q_LORA_RANK, MLA_HEADS * QK_HEAD_DIM), Q_LORA_RANK),
        'g_q_nope': g(ks[16], (N_B_LAYERS, QK_NOPE_DIM)),
        'g_q_rope': g(ks[17], (N_B_LAYERS, QK_ROPE_DIM)),
        'kv_norm_g': g(ks[18], (D_MODEL,)),
        'w_dkv': w(ks[19], (D_MODEL, KV_LORA_RANK), D_MODEL),
        'g_ckv': g(ks[20], (KV_LORA_RANK,)),
        'w_ukv': w(ks[21], (KV_LORA_RANK, MLA_HEADS * (QK_NOPE_DIM + V_HEAD_DIM)), KV_LORA_RANK),
        'w_kr': w(ks[22], (D_MODEL, QK_ROPE_DIM), D_MODEL),
        'g_k_nope': g(ks[23], (QK_NOPE_DIM,)),
        'g_k_rope': g(ks[24], (QK_ROPE_DIM,)),
    }


def reference(x, mem, positions, norm_g, ffn_w13, ffn_w2, w_out, mem_norm_g, w_mem_kv, g_mem_q, g_mem_k,
              conv_w_in, conv_w, mla_w_in, g_q_lora, w_uq, g_q_nope, g_q_rope,
              kv_norm_g, w_dkv, g_ckv, w_ukv, w_kr, g_k_nope, g_k_rope):
    cos, sin = rope_tables(positions)
    shared = None
    for layer in range(DEPTH):
        if layer == N_A_LAYERS:
            shared = shared_mla_kv(x, kv_norm_g, w_dkv, g_ckv, w_ukv, w_kr, g_k_nope, g_k_rope, cos, sin)
        x = x + FFN_HALF * swiglu(rms_norm(x, norm_g[layer, 0]), ffn_w13[layer, 0], ffn_w2[layer, 0])
        h = rms_norm(x, norm_g[layer, 1])
        if layer < N_A_LAYERS:
            y_mix, q_mem = short_conv_mixer(h, conv_w_in[layer], conv_w[layer])
        else:
            j = layer - N_A_LAYERS
            q_nope, q_rope, q_mem = mla_queries(h, mla_w_in[j], g_q_lora[j], w_uq[j], g_q_nope[j], g_q_rope[j], cos, sin)
            k_nope, k_rope, v = shared
            y_mix = causal_mla_attention(q_nope, q_rope, k_nope, k_rope, v)
        y_mem = memory_attention(q_mem, mem, mem_norm_g[layer], w_mem_kv[layer], g_mem_q[layer], g_mem_k[layer])
        x = x + jnp.concatenate([y_mix, y_mem], axis=-1) @ w_out[layer]
        x = x + FFN_HALF * swiglu(rms_norm(x, norm_g[layer, 2]), ffn_w13[layer, 1], ffn_w2[layer, 1])
    return x
```

```python
import numpy as np
from contextlib import ExitStack
import concourse.bass as bass
import concourse.mybir as mybir
from concourse.bass_utils import run_bass_kernel_spmd

F32 = mybir.dt.float32
BF16 = mybir.dt.bfloat16
I32 = mybir.dt.int32
AF = mybir.ActivationFunctionType
ALU = mybir.AluOpType

D = 1024
DFF = 2816
NJ = DFF // 128
DEPTH = 4
NA = 2
T = 1024
GS = 512
NG = T // GS
EPS = 1e-6
MEMT = 256
NH = 6
WSLOT = 2816
TWO_PI = 6.283185307179586
C1 = 6.28125
C2 = TWO_PI - C1


class Res:
    __slots__ = ("name", "last_w", "rd_c", "rd_d")

    def __init__(self, name):
        self.name = name
        self.last_w = None
        self.rd_c = {}
        self.rd_d = []


class Op:
    __slots__ = ("eng", "fn", "deps", "is_dma", "chan", "cnt", "idx", "seq", "sig", "waits")


class Sched:
    ENGS = ("pe", "act", "dve", "pool", "sp")

    def __init__(self):
        self.ops = []
        self.chan_cnt = {}
        self.proxy = None

    def add(self, eng, fn, reads=(), writes=(), chan=None):
        if eng == "sp" and fn is not None and self.proxy is not None:
            pfn, pres = self.proxy
            p = self.add("dve", pfn, list(reads) + [], list(writes) + [pres])
            op = Op()
            op.eng = eng
            op.fn = fn
            op.is_dma = True
            op.chan = chan
            op.sig = False
            self.chan_cnt[chan] = self.chan_cnt.get(chan, 0) + 1
            op.cnt = self.chan_cnt[chan]
            op.deps = [p]
            for r in reads:
                r.rd_d.append(op)
            for w in writes:
                w.last_w = op
                w.rd_c = {}
                w.rd_d = []
            op.idx = len(self.ops)
            self.ops.append(op)
            return op
        op = Op()
        op.eng = eng
        op.fn = fn
        op.is_dma = chan is not None
        op.chan = chan
        op.sig = False
        op.cnt = 0
        if chan is not None:
            self.chan_cnt[chan] = self.chan_cnt.get(chan, 0) + 1
            op.cnt = self.chan_cnt[chan]
        deps = []
        for r in reads:
            if r.last_w is not None:
                deps.append(r.last_w)
        for w in writes:
            if w.last_w is not None:
                deps.append(w.last_w)
            deps.extend(w.rd_c.values())
            deps.extend(w.rd_d)
        for r in reads:
            if op.is_dma:
                r.rd_d.append(op)
            else:
                r.rd_c[eng] = op
        for w in writes:
            w.last_w = op
            w.rd_c = {}
            w.rd_d = []
        op.deps = deps
        op.idx = len(self.ops)
        self.ops.append(op)
        return op

    def resolve(self):
        seq = {e: 0 for e in self.ENGS}
        byseq = {e: [] for e in self.ENGS}
        for op in self.ops:
            op.seq = seq[op.eng]
            seq[op.eng] += 1
            byseq[op.eng].append(op)
        wm = {e: {} for e in self.ENGS}
        wmd = {e: {} for e in self.ENGS}
        for op in self.ops:
            wc = {}
            wd = {}
            for d in op.deps:
                if d is op:
                    continue
                if d.is_dma:
                    if wd.get(d.chan, 0) < d.cnt:
                        wd[d.chan] = d.cnt
                else:
                    if d.eng == op.eng and op.eng == "pe":
                        continue
                    if wc.get(d.eng, -1) < d.seq:
                        wc[d.eng] = d.seq
            final = []
            for pe_, s in wc.items():
                if wm[op.eng].get(pe_, -1) >= s:
                    continue
                wm[op.eng][pe_] = s
                byseq[pe_][s].sig = True
                final.append(("c", pe_, s))
            for ch, c in wd.items():
                if wmd[op.eng].get(ch, 0) >= c:
                    continue
                wmd[op.eng][ch] = c
                final.append(("d", ch, c))
            op.waits = final
        self.rank = {e: {} for e in self.ENGS}
        for e in self.ENGS:
            r = 0
            for op in byseq[e]:
                if op.sig and not op.is_dma:
                    r += 1
                    self.rank[e][op.seq] = r
        self.byseq = byseq

    def emit(self, nc, stack):
        self.resolve()
        esem = {e: stack.enter_context(nc.semaphore("s_" + e)) for e in self.ENGS}
        csem = {ch: stack.enter_context(nc.semaphore("c_%s" % (ch,))) for ch in self.chan_cnt}
        block = stack.enter_context(nc.Block())
        rank = self.rank

        def run(e, eng):
            for op in self.byseq[e]:
                for kind, a, b in op.waits:
                    if kind == "c":
                        eng.wait_ge(esem[a], rank[a][b])
                    else:
                        eng.wait_ge(csem[a], 16 * b)
                if op.fn is None:
                    continue
                ins = op.fn(eng)
                if op.is_dma:
                    ins.then_inc(csem[op.chan], 16)
                elif op.sig:
                    ins.then_inc(esem[e], 1)

        @block.tensor
        def _(eng):
            run("pe", eng)

        @block.scalar
        def _(eng):
            run("act", eng)

        @block.vector
        def _(eng):
            run("dve", eng)

        @block.gpsimd
        def _(eng):
            run("pool", eng)

        @block.sync
        def _(eng):
            run("sp", eng)


def blockify(w, ncols):
    K, N = w.shape
    KC = K // 128
    nb = N // ncols
    return np.ascontiguousarray(
        w.reshape(KC, 128, nb, ncols).transpose(2, 1, 0, 3).reshape(nb, 128, KC * ncols))


def _swap64(a):
    return np.concatenate([a[..., 32:], a[..., :32]], axis=-1)


class CstPack:
    def __init__(self):
        self.cols = []
        self.idx = {}

    def add(self, name, col):
        self.idx[name] = len(self.cols)
        c = np.zeros(128, np.float32)
        c[: len(col)] = col
        self.cols.append(c)

    def add_vec(self, name, v):
        n = len(v) // 128
        for c in range(n):
            self.add("%s.%d" % (name, c), v[c * 128:(c + 1) * 128])

    def array(self):
        return np.ascontiguousarray(np.stack(self.cols, axis=1))


def prep_shared(inp):
    f = lambda a: np.asarray(a, dtype=np.float32)
    out = {}
    w13 = f(inp["ffn_w13"])
    w2 = f(inp["ffn_w2"])
    b13, b2 = [], []
    for l in range(DEPTH):
        for k in range(2):
            w = w13[l, k].reshape(D, 2, NJ, 128).transpose(0, 2, 1, 3).reshape(D, 2 * DFF)
            b13.append(blockify(w, 256))
            b2.append(blockify(w2[l, k], 128))
    out["w13"] = np.concatenate(b13, 0)
    out["w2"] = np.concatenate(b2, 0)
    cw = f(inp["conv_w_in"])
    bc = []
    for l in range(NA):
        order = []
        for c in range(6):
            order += [c, 6 + c, 12 + c]
        order += [18, 19]
        w = cw[l].reshape(D, 20, 128)[:, order, :].reshape(D, 2560)
        bc.append(blockify(w, 256))
    out["wcin"] = np.concatenate(bc, 0)
    out["wout"] = np.concatenate([blockify(f(inp["w_out"])[l], 256) for l in range(DEPTH)], 0)
    out["wmkv"] = np.concatenate([blockify(f(inp["w_mem_kv"])[l], 256) for l in range(DEPTH)], 0)
    mw = f(inp["mla_w_in"])
    bm = []
    for j in range(2):
        w = np.concatenate([mw[j], np.zeros((D, 128), np.float32)], axis=1)
        bm.append(blockify(w, 256))
    out["wmin"] = np.concatenate(bm, 0)
    uq = f(inp["w_uq"])
    bu = []
    for j in range(2):
        cols = []
        for h in range(NH):
            nope = uq[j][:, h * 192: h * 192 + 128]
            rope = uq[j][:, h * 192 + 128: h * 192 + 192]
            cols += [nope, rope, _swap64(rope)]
        bu.append(blockify(np.concatenate(cols, axis=1), 256))
    out["wuq"] = np.concatenate(bu, 0)
    out["wdkv"] = blockify(f(inp["w_dkv"]), 256)
    kr = f(inp["w_kr"])
    out["wkr"] = blockify(np.concatenate([kr, _swap64(kr)], axis=1), 128)
    ukv = f(inp["w_ukv"]).reshape(256, NH, 2, 128)
    out["wukvk"] = blockify(np.ascontiguousarray(ukv[:, :, 0, :]).reshape(256, 768), 256)
    out["wukvv"] = blockify(np.ascontiguousarray(ukv[:, :, 1, :]).reshape(256, 768), 768)

    cp = CstPack()
    ng = f(inp["norm_g"])
    for l in range(DEPTH):
        for k in range(3):
            cp.add_vec("ng%d%d" % (l, k), ng[l, k])
        cp.add_vec("mng%d" % l, f(inp["mem_norm_g"])[l])
        cp.add("gmq%d" % l, np.tile(f(inp["g_mem_q"])[l], 2))
        cp.add("gmk%d" % l, np.tile(f(inp["g_mem_k"])[l], 2))
    for l in range(NA):
        cwv = f(inp["conv_w"])[l]
        for tap in range(3):
            for c in range(6):
                cp.add("cw%d.%d.%d" % (l, tap, c), cwv[tap, c * 128:(c + 1) * 128])
    for j in range(2):
        cp.add_vec("gql%d" % j, f(inp["g_q_lora"])[j])
        cp.add("gqn%d" % j, f(inp["g_q_nope"])[j])
        gr = f(inp["g_q_rope"])[j]
        cp.add("gqr%d" % j, np.tile(gr, 2))
        cp.add("gqrs%d" % j, np.tile(_swap64(gr), 2))
    cp.add_vec("kvng", f(inp["kv_norm_g"]))
    cp.add_vec("gckv", f(inp["g_ckv"]))
    cp.add("gkn", f(inp["g_k_nope"]))
    gkr = f(inp["g_k_rope"])
    cp.add("gkr", np.tile(gkr, 2))
    cp.add("gkrs", np.tile(_swap64(gkr), 2))
    invf = (10000.0 ** (-np.arange(0, 64, 2, dtype=np.float32) / np.float32(64))).astype(np.float32)
    cp.add("invf", np.tile(invf, 4))
    cp.add("sgn", np.tile(np.concatenate([-np.ones(32, np.float32), np.ones(32, np.float32)]), 2))
    out["cst"] = cp.array()
    ident = np.eye(128, dtype=np.float32)
    ones = np.ones((128, 128), np.float32)
    bd = np.zeros((128, 128), np.float32)
    bd[:64, :64] = 1
    bd[64:, 64:] = 1
    tri = (np.arange(128)[None, :] >= np.arange(128)[:, None]).astype(np.float32)
    out["cmat"] = np.ascontiguousarray(np.concatenate([ident, ones, bd, tri], axis=1))
    return out, cp.idx


WSPEC = {
    "w13": (DEPTH * 2 * NJ, 2048),
    "w2": (DEPTH * 2 * 8, 2816),
    "wcin": (NA * 10, 2048),
    "wout": (DEPTH * 4, 2048),
    "wmkv": (DEPTH * 2, 2048),
    "wmin": (2 * 3, 2048),
    "wuq": (2 * NH, 768),
    "wdkv": (1, 2048),
    "wkr": (1, 1024),
    "wukvk": (3, 512),
    "wukvv": (1, 1536),
}


class Prog:
    def __init__(self, S, cidx, ncst, depth=DEPTH, nring=5):
        self.S = S
        self.NT = S // T
        self.cidx = cidx
        self.depth = depth
        self.nring = nring
        self.nc = bass.Bass("TRN2", target_bir_lowering=False)
        self.sch = Sched()
        self.ncst = ncst

    def mm(self, out, lhsT, rhs, start, stop, reads, writes):
        self.sch.add("pe", lambda e: e.matmul(out, lhsT=lhsT, rhs=rhs, start=start, stop=stop),
                     reads, writes)

    def tr(self, out, in_, reads, writes):
        ident = self.ident
        self.sch.add("pe", lambda e: e.transpose(out, in_, ident), reads + [self.r_cm], writes)

    def act(self, out, in_, func, reads, writes, scale=None, bias=None):
        kw = {}
        if scale is not None:
            kw["scale"] = scale
        if bias is not None:
            kw["bias"] = bias
        self.sch.add("act", lambda e: e.activation(out, in_, func, **kw), reads, writes)

    def stt(self, out, in0, scalar, in1, op0, op1, reads, writes, eng="dve"):
        self.sch.add(eng, lambda e: e.scalar_tensor_tensor(out, in0, scalar, in1, op0, op1), reads, writes)

    def tt(self, out, in0, in1, op, reads, writes, eng="dve"):
        self.sch.add(eng, lambda e: e.tensor_tensor(out, in0, in1, op), reads, writes)

    def ts(self, out, in0, s1, s2, op0, op1, reads, writes, eng="dve"):
        if op1 is None:
            self.sch.add(eng, lambda e: e.tensor_scalar(out, in0, s1, None, op0), reads, writes)
        else:
            self.sch.add(eng, lambda e: e.tensor_scalar(out, in0, s1, s2, op0, op1), reads, writes)

    def cp(self, out, in_, reads, writes, eng="dve"):
        self.sch.add(eng, lambda e: e.tensor_copy(out, in_), reads, writes)

    def recip(self, out, in_, reads, writes):
        self.sch.add("dve", lambda e: e.reciprocal(out, in_), reads, writes)

    def dma(self, q, out, in_, reads, writes, chan):
        return self.sch.add(q, lambda e: e.dma_start(out=out, in_=in_), reads, writes, chan=chan)

    def C(self, name):
        i = self.cidx[name]
        return self.cst[:, i:i + 1]

    def bank(self, pool="all"):
        if pool == "all":
            while self._rr in self._held:
                self._rr = (self._rr + 1) % 8
            b = self._rr
            self._rr = (self._rr + 1) % 8
        elif pool == "sc":
            b = self._rs
            self._rs = (self._rs + 1) % 4
        else:
            raise ValueError(pool)
        return b

    def hold(self, *bs):
        for b in bs:
            self._held.add(b)

    def release(self, *bs):
        for b in bs:
            self._held.discard(b)

    def pb(self, b, n=GS, p0=0, p1=128, c0=0):
        return self.ps[p0:p1, b * 512 + c0: b * 512 + c0 + n]

    def tmpf(self):
        i = self._tf
        self._tf = (self._tf + 1) % len(self.tf)
        return self.tf[i], self.r_tf[i]

    def tmpb(self):
        i = self._tb
        self._tb = (self._tb + 1) % len(self.tb)
        return self.tb[i], self.r_tb[i]

    def wload(self, wname, blk):
        key = (wname, blk)
        if key in self._wcache:
            return self._wcache[key]
        s = self._wr
        self._wr = (self._wr + 1) % self.nring
        width = WSPEC[wname][1]
        src = self.wbf[wname]
        dst = self.ring[s]
        rres = self.r_ring[s]
        self.dma("sp", dst[:, 0:width], src[blk, :, :], [self.r_wconv[key]], [rres], chan=("ring", s))
        for k in [k for k, v in self._wcache.items() if v[2] == s]:
            del self._wcache[k]
        self._wcache[key] = (dst, rres, s)
        return self._wcache[key]

    def sumsq_rstd(self, srcs, src_res, nfeat, lhs_ones, npart=128):
        sqs = []
        for a, r in zip(srcs, src_res):
            sq, rsq = self.tmpb()
            self.act(sq[0:npart, :], a, AF.Square, [r], [rsq])
            sqs.append((sq, rsq))
        b = self.bank()
        n = len(sqs)
        for i, (sq, rsq) in enumerate(sqs):
            self.mm(self.pb(b, p1=npart), lhs_ones, sq[0:npart, :], i == 0, i == n - 1,
                    [rsq, self.r_cm], [self.r_ps[b]])
        t1, r1 = self.tmpf()
        self.act(t1[0:npart, :], self.pb(b, p1=npart), AF.Sqrt, [self.r_ps[b]], [r1],
                 scale=1.0 / nfeat, bias=self.epsb[0:npart, :])
        t2, r2 = self.tmpf()
        self.recip(t2[0:npart, :], t1[0:npart, :], [r1], [r2])
        return t2, r2

    def rmsnorm_x(self, gname):
        for g in range(NG):
            gs = slice(g * GS, (g + 1) * GS)
            rstd, rr = self.sumsq_rstd([self.xT[:, c, gs] for c in range(8)],
                                       [self.r_x[c][g] for c in range(8)], D, self.ones)
            for c in range(8):
                self.stt(self.hT[:, c, gs], self.xT[:, c, gs], self.C("%s.%d" % (gname, c)), rstd[:, :],
                         ALU.mult, ALU.mult, [self.r_x[c][g], rr, self.r_cst], [self.r_h[c][g]])

    def ffn(self, l, k):
        self.rmsnorm_x("ng%d%d" % (l, 0 if k == 0 else 2))
        base13 = (l * 2 + k) * NJ
        base2 = (l * 2 + k) * 8
        PF = 3
        for j in range(min(PF, NJ)):
            self.wload("w13", base13 + j)
        for j in range(NJ):
            if j + PF < NJ:
                self.wload("w13", base13 + j + PF)
            else:
                self.wload("w2", base2 + j + PF - NJ)
            wb, rw, _ = self.wload("w13", base13 + j)
            for g in range(NG):
                gs = slice(g * GS, (g + 1) * GS)
                bg = self.bank()
                bu = self.bank()
                for kc in range(8):
                    self.mm(self.pb(bg), wb[:, kc * 256: kc * 256 + 128], self.hT[:, kc, gs], kc == 0, kc == 7,
                            [rw, self.r_h[kc][g]], [self.r_ps[bg]])
                for kc in range(8):
                    self.mm(self.pb(bu), wb[:, kc * 256 + 128: kc * 256 + 256], self.hT[:, kc, gs], kc == 0, kc == 7,
                            [rw, self.r_h[kc][g]], [self.r_ps[bu]])
                sg, rsg = self.tmpf()
                self.act(sg[:, :], self.pb(bg), AF.Silu, [self.r_ps[bg]], [rsg])
                self.tt(self.slab[:, j, gs], self.pb(bu), sg[:, :], ALU.mult, [self.r_ps[bu], rsg],
                        [self.r_slab[j][g]])
        for n in range(8):
            if n + PF < 8:
                self.wload("w2", base2 + n + PF)
            wb, rw, _ = self.wload("w2", base2 + n)
            for g in range(NG):
                gs = slice(g * GS, (g + 1) * GS)
                b = self.bank()
                for kc in range(NJ):
                    self.mm(self.pb(b), wb[:, kc * 128:(kc + 1) * 128], self.slab[:, kc, gs], kc == 0, kc == NJ - 1,
                            [rw, self.r_slab[kc][g]], [self.r_ps[b]])
                self.stt(self.xT[:, n, gs], self.pb(b), 0.5, self.xT[:, n, gs], ALU.mult, ALU.add,
                         [self.r_ps[b], self.r_x[n][g]], [self.r_x[n][g]])

    def proj_chunk(self, wname, blk0, chunk, g, rhs_buf, rhs_res, kcn=8, cpb=2, m=128, coff=0):
        blk = blk0 + chunk // cpb
        wb, rw, _ = self.wload(wname, blk)
        off = (chunk % cpb) * 128 + coff
        stride = cpb * 128
        gs = slice(g * GS, (g + 1) * GS)
        b = self.bank()
        for kc in range(kcn):
            self.mm(self.pb(b, p1=m), wb[:, kc * stride + off: kc * stride + off + m], rhs_buf[:, kc, gs],
                    kc == 0, kc == kcn - 1, [rw, rhs_res[kc][g]], [self.r_ps[b]])
        return b

    def mem_attn(self, l, g, qbanks):
        gs = slice(g * GS, (g + 1) * GS)
        for ch in range(2):
            qb = qbanks[ch]
            self.hold(qb)
            rstd, rr = self.sumsq_rstd([self.pb(qb)], [self.r_ps[qb]], 64, self.bd)
            qn, rqn = self.tmpb()
            self.stt(qn[:, :], self.pb(qb), self.C("gmq%d" % l), rstd[:, :], ALU.mult, ALU.mult,
                     [self.r_ps[qb], rr, self.r_cst], [rqn])
            self.release(qb)
            bo = self.bank()
            self.hold(bo)
            bd_ = self.bank()
            self.hold(bd_)
            for hh in range(2):
                hm = ch * 2 + hh
                po = hh * 64
                pts = []
                for mt in range(2):
                    bs = self.bank()
                    self.mm(self.pb(bs), self.memK[po:po + 64, l, ch, mt * 128:(mt + 1) * 128], qn[po:po + 64, :],
                            True, True, [self.r_mem[l], rqn], [self.r_ps[bs]])
                    pt, rpt = self.tmpb()
                    self.act(pt[:, :], self.pb(bs), AF.Exp, [self.r_ps[bs]], [rpt], scale=0.125)
                    pts.append((pt, rpt))
                for mt in range(2):
                    pt, rpt = pts[mt]
                    self.mm(self.pb(bo, p0=po, p1=po + 64), self.memV[:, l, mt, hm * 64:(hm + 1) * 64], pt[:, :],
                            mt == 0, mt == 1, [self.r_mem[l], rpt], [self.r_ps[bo]])
                for mt in range(2):
                    pt, rpt = pts[mt]
                    self.mm(self.pb(bd_, p0=po, p1=po + 64), self.ones[:, 0:64], pt[:, :],
                            mt == 0, mt == 1, [self.r_cm, rpt], [self.r_ps[bd_]])
            rc, rrc = self.tmpf()
            self.recip(rc[:, :], self.pb(bd_), [self.r_ps[bd_]], [rrc])
            self.tt(self.y[:, 6 + ch, gs], self.pb(bo), rc[:, :], ALU.mult, [self.r_ps[bo], rrc],
                    [self.r_y[6 + ch][g]])
            self.release(bo, bd_)

    def out_proj(self, l):
        for g in range(NG):
            gs = slice(g * GS, (g + 1) * GS)
            for n in range(8):
                b = self.proj_chunk("wout", l * 4, n, g, self.y, self.r_y)
                self.tt(self.xT[:, n, gs], self.pb(b), self.xT[:, n, gs], ALU.add,
                        [self.r_ps[b], self.r_x[n][g]], [self.r_x[n][g]])

    def conv_layer(self, l):
        self.rmsnorm_x("ng%d1" % l)
        for g in range(NG):
            gs = slice(g * GS, (g + 1) * GS)
            for c in range(6):
                bgb = self.proj_chunk("wcin", l * 10, 3 * c + 0, g, self.hT, self.r_h)
                self.hold(bgb)
                bgc = self.proj_chunk("wcin", l * 10, 3 * c + 1, g, self.hT, self.r_h)
                self.hold(bgc)
                bxt = self.proj_chunk("wcin", l * 10, 3 * c + 2, g, self.hT, self.r_h)
                self.hold(bxt)
                t1, r1 = self.tmpf()
                self.cp(t1[:, :], self.pb(bxt), [self.r_ps[bxt]], [r1])
                ui = self._ui
                self._ui = (self._ui + 1) % 2
                u = self.ubuf[ui]
                ru = self.r_u[ui]
                cr = self.carry[:, (l * 6 + c) * 2:(l * 6 + c) * 2 + 2]
                rcr = self.r_carry[l * 6 + c]
                self.cp(u[:, 0:2], cr, [rcr], [ru])
                self.tt(u[:, 2:2 + GS], self.pb(bgc), t1[:, :], ALU.mult, [self.r_ps[bgc], r1, ru], [ru])
                cv, rcv = self.tmpf()
                self.ts(cv[:, :], u[:, 2:2 + GS], self.C("cw%d.2.%d" % (l, c)), None, ALU.mult, None,
                        [ru, self.r_cst], [rcv])
                self.stt(cv[:, :], u[:, 1:1 + GS], self.C("cw%d.1.%d" % (l, c)), cv[:, :], ALU.mult, ALU.add,
                         [ru, rcv, self.r_cst], [rcv])
                self.stt(cv[:, :], u[:, 0:GS], self.C("cw%d.0.%d" % (l, c)), cv[:, :], ALU.mult, ALU.add,
                         [ru, rcv, self.r_cst], [rcv])
                self.tt(self.y[:, c, gs], self.pb(bgb), cv[:, :], ALU.mult, [self.r_ps[bgb], rcv],
                        [self.r_y[c][g]])
                self.cp(cr, u[:, GS:GS + 2], [ru], [rcr])
                self.release(bgb, bgc, bxt)
            qb = []
            for ch in range(2):
                qb.append(self.proj_chunk("wcin", l * 10, 18 + ch, g, self.hT, self.r_h))
                self.hold(qb[-1])
            self.mem_attn(l, g, qb)
            self.release(*qb)
        self.out_proj(l)

    def rope_tables(self, ti):
        t0 = ti * T
        for g in range(NG):
            gs = slice(g * GS, (g + 1) * GS)
            pi_ = self.posi
            self.dma("sp", pi_[:, :], self.pos_d[:, t0 + g * GS:t0 + (g + 1) * GS], [], [self.r_posi],
                     chan=("pos", 0))
            angt, r_a = self.tmpf()
            a2t, r_b = self.tmpf()
            kft, r_c = self.tmpf()
            ang, a2, kf = angt[0:64, :], a2t[0:64, :], kft[0:64, :]
            ki = self.rt_i
            r_i = self.r_rti
            self.cp(ang, pi_[:, :], [self.r_posi], [r_a])
            self.ts(ang, ang, self.C("invf")[0:64, :], None, ALU.mult, None, [r_a, self.r_cst], [r_a])
            for which in range(2):
                if which == 1:
                    self.ts(a2, ang, float(np.pi / 2), None, ALU.add, None, [r_a], [r_b])
                else:
                    self.cp(a2, ang, [r_a], [r_b])
                self.ts(kf, a2, float(1.0 / TWO_PI), None, ALU.mult, None, [r_b], [r_c])
                self.cp(ki[:, :], kf, [r_c], [r_i])
                self.cp(kf, ki[:, :], [r_i], [r_c])
                self.stt(a2, kf, float(-C1), a2, ALU.mult, ALU.add, [r_c, r_b], [r_b])
                self.stt(a2, kf, float(-C2), a2, ALU.mult, ALU.add, [r_c, r_b], [r_b])
                self.ts(kf, a2, float(np.pi), float(-TWO_PI), ALU.is_gt, ALU.mult, [r_b], [r_c])
                self.tt(a2, a2, kf, ALU.add, [r_b, r_c], [r_b])
                self.ts(kf, a2, float(-np.pi), float(TWO_PI), ALU.is_lt, ALU.mult, [r_b], [r_c])
                self.tt(a2, a2, kf, ALU.add, [r_b, r_c], [r_b])
                self.ts(a2, a2, float(np.pi), float(-np.pi), ALU.min, ALU.max, [r_b], [r_b])
                if which == 0:
                    self.act(self.SS[:, gs], a2, AF.Sin, [r_b], [self.r_SS])
                    self.ts(self.SS[:, gs], self.SS[:, gs], self.C("sgn")[0:64, :], None, ALU.mult, None,
                            [self.r_SS, self.r_cst], [self.r_SS])
                else:
                    self.act(self.CC[:, gs], a2, AF.Sin, [r_b], [self.r_CC])

    def rope_norm(self, b_r, b_s, gname, gsname, g, out_ap, out_res):
        gs = slice(g * GS, (g + 1) * GS)
        rstd, rr = self.sumsq_rstd([self.pb(b_r, p1=64)], [self.r_ps[b_r]], 64, self.ones[0:64, 0:64], npart=64)
        t1, r1 = self.tmpf()
        self.stt(t1[0:64, :], self.pb(b_r, p1=64), self.C(gname)[0:64, :], self.CC[:, gs], ALU.mult, ALU.mult,
                 [self.r_ps[b_r], self.r_CC, self.r_cst], [r1])
        t2, r2 = self.tmpf()
        self.stt(t2[0:64, :], self.pb(b_s, p1=64), self.C(gsname)[0:64, :], self.SS[:, gs], ALU.mult, ALU.mult,
                 [self.r_ps[b_s], self.r_SS, self.r_cst], [r2])
        self.tt(t1[0:64, :], t1[0:64, :], t2[0:64, :], ALU.add, [r1, r2], [r1])
        self.tt(out_ap, t1[0:64, :], rstd[0:64, :], ALU.mult, [r1, rr], [out_res])

    def shared_kv(self, ti):
        t0 = ti * T
        self.rmsnorm_x("kvng")
        ckvn = lambda c: self.slab[:, 12 + c, :]
        r_ckvn = [self.r_slab[12], self.r_slab[13]]
        for g in range(NG):
            gs = slice(g * GS, (g + 1) * GS)
            bs_ = []
            for c in range(2):
                bs_.append(self.proj_chunk("wdkv", 0, c, g, self.hT, self.r_h))
                self.hold(bs_[-1])
            rstd, rr = self.sumsq_rstd([self.pb(b) for b in bs_], [self.r_ps[b] for b in bs_], 256, self.ones)
            for c in range(2):
                self.stt(ckvn(c)[:, gs], self.pb(bs_[c]), self.C("gckv.%d" % c), rstd[:, :], ALU.mult, ALU.mult,
                         [self.r_ps[bs_[c]], rr, self.r_cst], [r_ckvn[c][g]])
            self.release(*bs_)
            b_r = self.proj_chunk("wkr", 0, 0, g, self.hT, self.r_h, cpb=1, m=64, coff=0)
            self.hold(b_r)
            b_s = self.proj_chunk("wkr", 0, 0, g, self.hT, self.r_h, cpb=1, m=64, coff=64)
            self.hold(b_s)
            self.rope_norm(b_r, b_s, "gkr", "gkrs", g, self.slab[0:64, 14, gs], self.r_slab[14][g])
            self.release(b_r, b_s)
            ckv_buf = self.slab[:, 12:14, :]
            for h in range(NH):
                b = self.proj_chunk("wukvk", 0, h, g, ckv_buf, r_ckvn, kcn=2)
                self.hold(b)
                rstd, rr = self.sumsq_rstd([self.pb(b)], [self.r_ps[b]], 128, self.ones)
                self.stt(self.slab[:, h, gs], self.pb(b), self.C("gkn"), rstd[:, :], ALU.mult, ALU.mult,
                         [self.r_ps[b], rr, self.r_cst], [self.r_slab[h][g]])
                self.release(b)
            wv, rwv, _ = self.wload("wukvv", 0)
            for tt_ in range(4):
                tok = slice(g * GS + tt_ * 128, g * GS + (tt_ + 1) * 128)
                kt = g * 4 + tt_
                for part, (n0, nn) in enumerate(((0, 512), (512, 256))):
                    b = self.bank()
                    for kc in range(2):
                        self.mm(self.pb(b, n=nn), ckvn(kc)[:, tok], wv[:, kc * 768 + n0: kc * 768 + n0 + nn],
                                kc == 0, kc == 1, [rwv, r_ckvn[kc][g]], [self.r_ps[b]])
                    nh = nn // 128
                    h0 = n0 // 128
                    for hh in range(nh):
                        h = h0 + hh
                        eng = "dve" if (hh % 2 == 0) else "act"
                        dst = self.slab[:, 6 + h, kt * 128:(kt + 1) * 128]
                        src = self.pb(b, n=128, c0=hh * 128)
                        self.cp(dst, src, [self.r_ps[b]], [self.r_slab[6 + h][g]])
        rkv = self.r_kv[ti]
        for h in range(NH):
            self.dma("sp", self.knT_d[h, :, t0:t0 + T], self.slab[:, h, :],
                     [self.r_slab[h][0], self.r_slab[h][1]], [rkv], chan=("kvst", ti))
            self.dma("sp", self.v_d[h, :, ti * 8:(ti + 1) * 8, :],
                     self.slab[:, 6 + h, :].rearrange("p (k d) -> p k d", d=128),
                     [self.r_slab[6 + h][0], self.r_slab[6 + h][1]], [rkv], chan=("kvst", ti))
        self.dma("sp", self.krT_d[:, t0:t0 + T], self.slab[0:64, 14, :],
                 [self.r_slab[14][0], self.r_slab[14][1]], [rkv], chan=("kvst", ti))

    def mla_layer(self, l, ti):
        j = l - NA
        t0 = ti * T
        self.rmsnorm_x("ng%d1" % l)
        qbanks_all = []
        for g in range(NG):
            gs = slice(g * GS, (g + 1) * GS)
            bs_ = []
            for c in range(3):
                bs_.append(self.proj_chunk("wmin", j * 3, c, g, self.hT, self.r_h))
                self.hold(bs_[-1])
            rstd, rr = self.sumsq_rstd([self.pb(b) for b in bs_], [self.r_ps[b] for b in bs_], 384, self.ones)
            for c in range(3):
                self.stt(self.cqn[:, c, gs], self.pb(bs_[c]), self.C("gql%d.%d" % (j, c)), rstd[:, :],
                         ALU.mult, ALU.mult, [self.r_ps[bs_[c]], rr, self.r_cst], [self.r_cqn[c][g]])
            self.release(*bs_)
            qb = []
            for ch in range(2):
                qb.append(self.proj_chunk("wmin", j * 3, 3 + ch, g, self.hT, self.r_h))
                self.hold(qb[-1])
            self.mem_attn(l, g, qb)
            self.release(*qb)
            for h in range(NH):
                wb, rw, _ = self.wload("wuq", j * NH + h)
                bn = self.bank()
                self.hold(bn)
                br = self.bank()
                self.hold(br)
                bsw = self.bank()
                self.hold(bsw)
                for (b, off, m) in ((bn, 0, 128), (br, 128, 64), (bsw, 192, 64)):
                    for kc in range(3):
                        self.mm(self.pb(b, p1=m), wb[:, kc * 256 + off: kc * 256 + off + m], self.cqn[:, kc, gs],
                                kc == 0, kc == 2, [rw, self.r_cqn[kc][g]], [self.r_ps[b]])
                rstd, rr = self.sumsq_rstd([self.pb(bn)], [self.r_ps[bn]], 128, self.ones)
                self.stt(self.slab[:, h, gs], self.pb(bn), self.C("gqn%d" % j), rstd[:, :], ALU.mult, ALU.mult,
                         [self.r_ps[bn], rr, self.r_cst], [self.r_slab[h][g]])
                self.rope_norm(br, bsw, "gqr%d" % j, "gqrs%d" % j, g, self.slab[0:64, 6 + h, gs],
                               self.r_slab[6 + h][g])
                self.release(bn, br, bsw)
        nkt = (t0 + T) // 128
        nkb = (nkt + 7) // 8
        scale = float(192 ** -0.5)
        for h in range(NH):
            acc_o = [4, 5]
            acc_d = [6, 7]
            first = [True, True]
            pend = []

            last_kg = [(t0 + g * GS + GS - 1) // 128 for g in range(NG)]

            def flush(n_keep):
                while len(pend) > n_keep:
                    g, vt, rkvs, c0, pt, rpt, kg_ = pend.pop(0)
                    st = first[g]
                    first[g] = False
                    sp_ = (kg_ == last_kg[g])
                    self.mm(self.pb(acc_o[g], n=GS - c0, c0=c0), vt, pt[:, c0:GS], st, sp_,
                            [rkvs, rpt], [self.r_ps[acc_o[g]]])
                    self.mm(self.pb(acc_d[g], n=GS - c0, c0=c0), self.ones, pt[:, c0:GS], st, sp_,
                            [self.r_cm, rpt], [self.r_ps[acc_d[g]]])

            for kb in range(nkb):
                ks = self._kvs
                self._kvs = (self._kvs + 1) % 3
                sl0 = 12 + ks * 3
                rks = [self.r_slab[sl0][0], self.r_slab[sl0][1], self.r_slab[sl0 + 1][0], self.r_slab[sl0 + 1][1],
                       self.r_slab[sl0 + 2][0], self.r_slab[sl0 + 2][1]]
                k0 = kb * 1024
                ch = ("kvld", ks)
                self.dma("sp", self.slab[:, sl0, :], self.knT_d[h, :, k0:k0 + 1024], [self.r_kv[kb]], rks, chan=ch)
                self.dma("sp", self.slab[:, sl0 + 1, :].rearrange("p (k d) -> p k d", d=128),
                         self.v_d[h, :, kb * 8:(kb + 1) * 8, :], [self.r_kv[kb]], rks, chan=ch)
                self.dma("sp", self.slab[0:64, sl0 + 2, :], self.krT_d[:, k0:k0 + 1024], [self.r_kv[kb]], rks, chan=ch)
                rk = rks[0]
                for kti in range(8):
                    kg = kb * 8 + kti
                    kpos = kg * 128
                    for g in range(NG):
                        qlo = t0 + g * GS
                        if kpos > qlo + GS - 1:
                            continue
                        c0 = max(0, kpos - qlo)
                        b = self.bank("sc")
                        self.mm(self.pb(b, n=GS - c0, c0=c0), self.slab[:, sl0, kti * 128:(kti + 1) * 128],
                                self.slab[:, h, g * GS + c0:(g + 1) * GS], True, False,
                                [rk, self.r_slab[h][g]], [self.r_ps[b]])
                        self.mm(self.pb(b, n=GS - c0, c0=c0), self.slab[0:64, sl0 + 2, kti * 128:(kti + 1) * 128],
                                self.slab[0:64, 6 + h, g * GS + c0:(g + 1) * GS], False, True,
                                [rk, self.r_slab[6 + h][g]], [self.r_ps[b]])
                        pt, rpt = self.ptile()
                        self.act(pt[:, c0:GS], self.pb(b, n=GS - c0, c0=c0), AF.Exp, [self.r_ps[b]], [rpt],
                                 scale=scale)
                        if kpos >= qlo:
                            self.tt(pt[:, c0:c0 + 128], pt[:, c0:c0 + 128], self.tri, ALU.mult,
                                    [rpt, self.r_cm], [rpt], eng="dve")
                        vt = self.slab[:, sl0 + 1, kti * 128:(kti + 1) * 128]
                        pend.append((g, vt, rk, c0, pt, rpt, kg))
                        flush(2)
            flush(0)
            for g in range(NG):
                gs = slice(g * GS, (g + 1) * GS)
                rc, rrc = self.tmpf()
                self.recip(rc[:, :], self.pb(acc_d[g]), [self.r_ps[acc_d[g]]], [rrc])
                self.tt(self.y[:, h, gs], self.pb(acc_o[g]), rc[:, :], ALU.mult, [self.r_ps[acc_o[g]], rrc],
                        [self.r_y[h][g]])
        self.out_proj(l)

    def ptile(self):
        i = self._pt
        self._pt = (self._pt + 1) % len(self.pts)
        return self.pts[i], self.r_pts[i]

    def load_x(self, ti):
        t0 = ti * T
        import os
        for tt_ in range(int(os.environ.get("KNTT", T // 128))):
            s = self._xs
            self._xs = (self._xs + 1) % 2
            xs = self.xs[s]
            rxs = self.r_xs[s]
            self.dma("sp", xs[:, :], self.x_d[t0 + tt_ * 128: t0 + (tt_ + 1) * 128, :], [], [rxs], chan=("xs", s))
            g = tt_ // 4
            for half in range(2):
                b = self.bank()
                for cc in range(4):
                    c = half * 4 + cc
                    self.tr(self.pb(b, n=128, c0=cc * 128), xs[:, c * 128:(c + 1) * 128], [rxs], [self.r_ps[b]])
                for cc in range(4):
                    c = half * 4 + cc
                    dst = self.xT[:, c, tt_ * 128:(tt_ + 1) * 128]
                    src = self.pb(b, n=128, c0=cc * 128)
                    self.cp(dst, src, [self.r_ps[b]], [self.r_x[c][g]])

    def store_x(self, ti):
        t0 = ti * T
        for tt_ in range(T // 128):
            s = self._xs
            self._xs = (self._xs + 1) % 2
            xs = self.xs[s]
            rxs = self.r_xs[s]
            g = tt_ // 4
            for half in range(2):
                b = self.bank()
                for cc in range(4):
                    c = half * 4 + cc
                    self.tr(self.pb(b, n=128, c0=cc * 128), self.xT[:, c, tt_ * 128:(tt_ + 1) * 128],
                            [self.r_x[c][g]], [self.r_ps[b]])
                self.cp(xs[:, half * 512:(half + 1) * 512], self.pb(b), [self.r_ps[b]], [rxs])
            op = self.dma("sp", self.out_d[t0 + tt_ * 128: t0 + (tt_ + 1) * 128, :], xs[:, :], [rxs],
                          [], chan=("ost", s))
            self.store_ops.append(op)

    def prologue_mem(self):
        memT = self.xT[:, :, 0:MEMT]
        for mt in range(2):
            s = self._xs
            self._xs = (self._xs + 1) % 2
            xs = self.xs[s]
            rxs = self.r_xs[s]
            self.dma("sp", xs[:, :], self.mem_d[mt * 128:(mt + 1) * 128, :], [], [rxs], chan=("xs", s))
            for half in range(2):
                b = self.bank()
                for cc in range(4):
                    c = half * 4 + cc
                    self.tr(self.pb(b, n=128, c0=cc * 128), xs[:, c * 128:(c + 1) * 128], [rxs], [self.r_ps[b]])
                for cc in range(4):
                    c = half * 4 + cc
                    self.cp(memT[:, c, mt * 128:(mt + 1) * 128], self.pb(b, n=128, c0=cc * 128), [self.r_ps[b]],
                            [self.r_x[c][0]])
        sqs = []
        b = self.bank()
        for c in range(8):
            sq, rsq = self.tmpb()
            self.act(sq[:, 0:MEMT], memT[:, c, :], AF.Square, [self.r_x[c][0]], [rsq])
            self.mm(self.pb(b, n=MEMT), self.ones, sq[:, 0:MEMT], c == 0, c == 7, [rsq, self.r_cm], [self.r_ps[b]])
        t1, r1 = self.tmpf()
        self.act(t1[:, 0:MEMT], self.pb(b, n=MEMT), AF.Sqrt, [self.r_ps[b]], [r1], scale=1.0 / D,
                 bias=self.epsb[:, :])
        self.recip(self.mrstd[:, :], t1[:, 0:MEMT], [r1], [self.r_mrstd])
        for l in range(self.depth):
            mn = self.hT
            for c in range(8):
                self.stt(mn[:, c, 0:MEMT], memT[:, c, :], self.C("mng%d.%d" % (l, c)), self.mrstd[:, :], ALU.mult,
                         ALU.mult, [self.r_x[c][0], self.r_mrstd, self.r_cst], [self.r_h[c][0]])
            wk, rwk, _ = self.wload("wmkv", l * 2)
            wv, rwv, _ = self.wload("wmkv", l * 2 + 1)
            for ch in range(2):
                b = self.bank()
                for kc in range(8):
                    self.mm(self.pb(b, n=MEMT), wk[:, kc * 256 + ch * 128: kc * 256 + (ch + 1) * 128],
                            mn[:, kc, 0:MEMT], kc == 0, kc == 7, [rwk, self.r_h[kc][0]], [self.r_ps[b]])
                sq, rsq = self.tmpb()
                self.act(sq[:, 0:MEMT], self.pb(b, n=MEMT), AF.Square, [self.r_ps[b]], [rsq])
                b2 = self.bank()
                self.mm(self.pb(b2, n=MEMT), self.bd, sq[:, 0:MEMT], True, True, [rsq, self.r_cm], [self.r_ps[b2]])
                t1, r1 = self.tmpf()
                self.act(t1[:, 0:MEMT], self.pb(b2, n=MEMT), AF.Sqrt, [self.r_ps[b2]], [r1], scale=1.0 / 64,
                         bias=self.epsb[:, :])
                t2, r2 = self.tmpf()
                self.recip(t2[:, 0:MEMT], t1[:, 0:MEMT], [r1], [r2])
                self.stt(self.memK[:, l, ch, :], self.pb(b, n=MEMT), self.C("gmk%d" % l), t2[:, 0:MEMT], ALU.mult,
                         ALU.mult, [self.r_ps[b], r2, self.r_cst], [self.r_mem[l]])
            for mt in range(2):
                b = self.bank()
                for kc in range(8):
                    self.mm(self.pb(b, n=256), mn[:, kc, mt * 128:(mt + 1) * 128],
                            wv[:, kc * 256: kc * 256 + 256], kc == 0, kc == 7, [rwv, self.r_h[kc][0]],
                            [self.r_ps[b]])
                self.cp(self.memV[:, l, mt, :], self.pb(b, n=256), [self.r_ps[b]], [self.r_mem[l]])

    def convert_weights(self):
        self.r_wconv = {}
        gid = [0]

        def conv(wname, b0, b1, step):
            for s in range(b0, b1, step):
                e = min(s + step, b1)
                ch = ("wc", gid[0])
                gid[0] += 1
                ops_ = []
                for b in range(s, e):
                    r = Res("wc")
                    ops_.append(self.dma("pool", self.wbf[wname][b, :, :], self.wf32[wname][b, :, :], [], [r],
                                         chan=ch))
                    self.r_wconv[(wname, b)] = r
                for b in range(s, e):
                    self.r_wconv[(wname, b)].last_w = ops_[-1]

        def ffnw(l, k, step):
            conv("w13", (l * 2 + k) * NJ, (l * 2 + k + 1) * NJ, step)
            conv("w2", (l * 2 + k) * 8, (l * 2 + k + 1) * 8, step)

        conv("wmkv", 0, self.depth * 2, 2)
        for l in range(self.depth):
            ffnw(l, 0, 2 if l == 0 else 11)
            if l < NA:
                conv("wcin", l * 10, (l + 1) * 10, 5)
            else:
                conv("wmin", (l - NA) * 3, (l - NA + 1) * 3, 3)
                conv("wuq", (l - NA) * NH, (l - NA + 1) * NH, 6)
            conv("wout", l * 4, (l + 1) * 4, 4)
            ffnw(l, 1, 4 if l == 0 else 11)
            if l == NA - 1:
                conv("wdkv", 0, 1, 1)
                conv("wkr", 0, 1, 1)
                conv("wukvk", 0, 3, 3)
                conv("wukvv", 0, 1, 1)

    def build(self):
        nc = self.nc
        S = self.S
        with ExitStack() as st:
            E = st.enter_context
            dt = lambda name, shape, dtype, kind: nc.dram_tensor(name, shape, dtype, kind=kind).ap()
            self.x_d = dt("x", [S, D], F32, "ExternalInput")
            self.mem_d = dt("mem", [MEMT, D], F32, "ExternalInput")
            self.pos_d = dt("pos", [64, S], I32, "ExternalInput")
            self.cst_d = dt("cst", [128, self.ncst], F32, "ExternalInput")
            self.cmat_d = dt("cmat", [128, 512], F32, "ExternalInput")
            self.out_d = dt("out", [S, D], F32, "ExternalOutput")
            self.wf32 = {}
            self.wbf = {}
            for name, (nb, wd) in WSPEC.items():
                self.wf32[name] = dt(name, [nb, 128, wd], F32, "ExternalInput")
                self.wbf[name] = dt(name + "_bf", [nb, 128, wd], BF16, "Internal")
            self.knT_d = dt("knT", [NH, 128, S], BF16, "Internal")
            self.v_d = dt("vsc", [NH, 128, S // 128, 128], BF16, "Internal")
            self.krT_d = dt("krT", [64, S], BF16, "Internal")

            sb = lambda name, shape, dtype: E(nc.sbuf_tensor(name, shape, dtype))
            self.xT = sb("xT", [128, 8, T], F32)
            self.hT = sb("hT", [128, 8, T], BF16)
            self.slab = sb("slab", [128, NJ, T], BF16)
            self.y = sb("y", [128, 8, T], BF16)
            self.ring = [sb("ring%d" % i, [128, WSLOT], BF16) for i in range(self.nring)]
            self.xs = [sb("xs%d" % i, [128, D], F32) for i in range(2)]
            self.CC = sb("CC", [64, T], F32)
            self.SS = sb("SS", [64, T], F32)
            self.tf = [sb("tf%d" % i, [128, GS], F32) for i in range(6)]
            self.tb = [sb("tb%d" % i, [128, GS], BF16) for i in range(8)]
            self.pts = [sb("pt%d" % i, [128, GS], BF16) for i in range(4)]
            self.ubuf = [sb("u%d" % i, [128, GS + 2], F32) for i in range(2)]
            self.carry = sb("carry", [128, NA * 6 * 2], F32)
            self.memK = sb("memK", [128, DEPTH, 2, MEMT], BF16)
            self.memV = sb("memV", [128, DEPTH, 2, 256], BF16)
            self.mrstd = sb("mrstd", [128, MEMT], F32)
            self.cst = sb("cst_sb", [128, self.ncst], F32)
            self.cmat = sb("cmat_sb", [128, 512], F32)
            self.cmb = sb("cmb", [128, 384], BF16)
            self.epsb = sb("epsb", [128, 1], F32)
            self.dummy = sb("pxdummy", [128, 1], F32)
            self.posi = sb("posi", [64, GS], I32)
            self.rt_i = sb("rt_i", [64, GS], I32)
            self.cqn = self.slab[:, 18:21, :]
            self.ps = E(nc.psum_tensor("ps", [128, 4096], F32))
            self.ident = self.cmat[:, 0:128]
            self.ones = self.cmb[:, 0:128]
            self.bd = self.cmb[:, 128:256]
            self.tri = self.cmb[:, 256:384]

            R = Res
            self.r_x = [[R("x") for _ in range(NG)] for _ in range(8)]
            self.r_h = [[R("h") for _ in range(NG)] for _ in range(8)]
            self.r_slab = [[R("sl") for _ in range(NG)] for _ in range(NJ)]
            self.r_y = [[R("y") for _ in range(NG)] for _ in range(8)]
            self.r_cqn = self.r_slab[18:21]
            self.r_ring = [R("ring") for _ in range(self.nring)]
            self.r_xs = [R("xs") for _ in range(2)]
            self.r_tf = [R("tf") for _ in self.tf]
            self.r_tb = [R("tb") for _ in self.tb]
            self.r_pts = [R("pt") for _ in self.pts]
            self.r_u = [R("u") for _ in range(2)]
            self.r_carry = [R("cr") for _ in range(NA * 6)]
            self.r_mem = [R("mem") for _ in range(DEPTH)]
            self.r_mrstd = R("mrstd")
            self.r_cst = R("cst")
            self.r_cm = R("cm")
            self.r_ps = [R("ps") for _ in range(8)]
            self.r_CC = R("CC")
            self.r_SS = R("SS")
            self.r_posi = R("posi")
            self.r_rti = R("i")
            self.r_kv = [R("kv") for _ in range(self.NT)]
            self.r_out = R("out")
            self._rr = self._rs = self._tf = self._tb = self._wr = self._ui = self._kvs = self._pt = self._xs = 0
            self._wcache = {}
            self._held = set()
            dummy = self.dummy
            self.sch.proxy = (lambda e: e.memset(dummy[:, :], 0.0), R("dummy"))
            self.store_ops = []

            self.dma("sp", self.cst[:, :], self.cst_d[:, :], [], [self.r_cst], chan=("c", 0))
            self.dma("sp", self.cmat[:, :], self.cmat_d[:, :], [], [self.r_cm], chan=("c", 1))
            self.cp(self.cmb[:, :], self.cmat[:, 128:512], [self.r_cm], [self.r_cm])
            self.sch.add("dve", lambda e: e.memset(self.epsb[:, :], EPS), [], [self.r_cm])
            self.sch.add("dve", lambda e: e.memset(self.carry[:, :], 0.0), [], self.r_carry)
            import os
            stage = int(os.environ.get("KSTAGE", "9"))
            self.convert_weights()
            if stage >= 1:
                self.prologue_mem()
            for ti in range(self.NT if stage >= 2 else 0):
                self.load_x(ti)
                if stage < 3:
                    if os.environ.get("KVAR", "") != "a":
                        self.store_x(ti)
                    continue
                if self.depth > NA:
                    self.rope_tables(ti)
                for l in range(self.depth):
                    if l == NA:
                        self.shared_kv(ti)
                    self.ffn(l, 0)
                    if l < NA:
                        self.conv_layer(l)
                    else:
                        self.mla_layer(l, ti)
                    self.ffn(l, 1)
                self.store_x(ti)
            fin = self.sch.add("dve", None, [], [])
            fin.deps.extend(self.store_ops)
            self.sch.emit(nc, st)
        return nc


_CACHE = {}


def _get_prog(S, cidx, ncst, depth=DEPTH):
    key = (S, ncst, depth)
    if key not in _CACHE:
        p = Prog(S, cidx, ncst, depth=depth)
        p.build()
        _CACHE[key] = p
    return _CACHE[key]


def run_cores(inp, S, ncores, depth=DEPTH, trace=False):
    shared, cidx = prep_shared(inp)
    prog = _get_prog(S, cidx, shared["cst"].shape[1], depth=depth)
    x = np.asarray(inp["x"], dtype=np.float32)
    mem = np.asarray(inp["mem"], dtype=np.float32)
    pos = np.asarray(inp["positions"]).astype(np.int32)
    in_maps = []
    for b in range(ncores):
        m = dict(shared)
        m["x"] = np.ascontiguousarray(x[b])
        m["mem"] = np.ascontiguousarray(mem[b])
        m["pos"] = np.ascontiguousarray(np.broadcast_to(pos[b][None, :], (64, S)))
        in_maps.append(m)
    res = run_bass_kernel_spmd(prog.nc, in_maps, core_ids=list(range(ncores)), trace=trace)
    out = np.stack([np.asarray(r["out"], dtype=np.float32) for r in res.results], axis=0)
    return out, res


def kernel(**inputs):
    x = inputs["x"]
    B, S, _ = x.shape
    out, _ = run_cores(inputs, S, B)
    return out
```
